# Optimizing a Trainium2 kernel written in Bass

```python
import jax, jax.numpy as jnp
from jax import lax
import numpy as np

D_MODEL = 1024
BATCH = 8
SEQ = 8192
DEPTH = 4

GRID_W = 64
CTX_LEN = 256
RMS_EPS = 1e-6
N_DIR = 2
REC_CHUNK = 64
A_HEADS = 4
A_HEAD_DIM = 3 * D_MODEL // 32
A_WIDTH = A_HEADS * A_HEAD_DIM
CONV_W = 3
B_GROUPS = 4
B_WIDTH = D_MODEL // 4
B_GROUP_DIM = B_WIDTH // B_GROUPS
B_CHUNK = 128
C_HEADS = 4
C_WIDTH = D_MODEL - A_WIDTH - B_WIDTH
C_HEAD_DV = C_WIDTH // C_HEADS
C_HEAD_DK = C_HEAD_DV // 2
C_KEY_WIDTH = C_HEADS * C_HEAD_DK
C_GATE_RANK = 16
C_GATE_NORM = 16.0
D_FF = ((8 * D_MODEL // 3 + 255) // 256) * 256
IN_SIZES = (A_WIDTH, A_WIDTH, A_WIDTH, A_WIDTH, N_DIR * 2 * A_HEADS,
            B_WIDTH, B_WIDTH,
            C_KEY_WIDTH, C_KEY_WIDTH, C_WIDTH, C_WIDTH, N_DIR * C_GATE_RANK)
N_IN = sum(IN_SIZES)

kernel_name = 'hybrid_mlstm_sgu_gla_dit'


def rmsnorm(x, g):
    xf = x.astype(jnp.float32)
    y = xf * lax.rsqrt(jnp.mean(xf * xf, axis=-1, keepdims=True) + RMS_EPS)
    return (y * g.astype(jnp.float32)).astype(x.dtype)


def modulate(h, shift, scale):
    return h * (1 + scale) + shift


def centred_dwconv(x, w):
    pad = CONV_W // 2
    t_len = x.shape[1]
    xp = jnp.pad(x, ((0, 0), (pad, pad), (0, 0)))
    return sum(w[j] * xp[:, j:j + t_len] for j in range(CONV_W))


def split_heads(a, n_heads):
    bsz, t_len, _ = a.shape
    return a.reshape(bsz, t_len, n_heads, -1).transpose(0, 2, 1, 3)


def to_chunks(a):
    t_len = a.shape[2]
    a = a.reshape(a.shape[:2] + (t_len // REC_CHUNK, REC_CHUNK) + a.shape[3:])
    return jnp.moveaxis(a, 2, 0)


def from_chunks(a):
    a = jnp.moveaxis(a, 0, 2)
    return a.reshape(a.shape[:2] + (a.shape[2] * a.shape[3],) + a.shape[4:])


def raster_to_column(h, rows):
    bsz, t_len, d = h.shape
    return h.reshape(bsz, rows, GRID_W, d).transpose(0, 2, 1, 3).reshape(bsz, t_len, d)


def column_to_raster(h, rows):
    bsz, t_len, d = h.shape
    return h.reshape(bsz, GRID_W, rows, d).transpose(0, 2, 1, 3).reshape(bsz, t_len, d)


def mlstm_scan(q, k, v, log_i, log_f, state):
    tril = jnp.tril(jnp.ones((REC_CHUNK, REC_CHUNK), dtype=bool))

    def step(carry, xs):
        C, n, m = carry
        qc, kc, vc, ic, fc = xs
        b = jnp.cumsum(fc, axis=-1)
        d_log = jnp.where(tril, b[..., :, None] - b[..., None, :] + ic[..., None, :], -jnp.inf)
        inter_log = b + m[..., None]
        m_t = jnp.maximum(inter_log, jnp.max(d_log, axis=-1))
        s = jnp.einsum('bhtd,bhsd->bhts', qc, kc) * jnp.exp(d_log - m_t[..., None])
        inter = jnp.exp(inter_log - m_t)
        num = jnp.einsum('bhts,bhse->bhte', s, vc) + inter[..., None] * jnp.einsum('bhed,bhtd->bhte', C, qc)
        den = jnp.sum(s, axis=-1) + inter * jnp.einsum('bhd,bhtd->bht', n, qc)
        h = num / jnp.maximum(jnp.abs(den), jnp.exp(-m_t))[..., None]
        b_last = b[..., -1]
        w_log = b_last[..., None] - b + ic
        m_new = jnp.maximum(b_last + m, jnp.max(w_log, axis=-1))
        decay = jnp.exp(b_last + m - m_new)
        w = jnp.exp(w_log - m_new[..., None])
        C_new = decay[..., None, None] * C + jnp.einsum('bhs,bhse,bhsd->bhed', w, vc, kc)
        n_new = decay[..., None] * n + jnp.einsum('bhs,bhsd->bhd', w, kc)
        return (C_new, n_new, m_new), h

    state, h = lax.scan(step, state, tuple(to_chunks(a) for a in (q, k, v, log_i, log_f)))
    return state, from_chunks(h)


def gla_scan(q, k, v, log_a, S):
    tril = jnp.tril(jnp.ones((REC_CHUNK, REC_CHUNK), dtype=bool))[:, :, None]

    def step(S, xs):
        qc, kc, vc, ac = xs
        b = jnp.cumsum(ac, axis=2)
        rel = jnp.where(tril, b[:, :, :, None, :] - b[:, :, None, :, :], -jnp.inf)
        att = jnp.einsum('bhtd,bhsd,bhtsd->bhts', qc, kc, jnp.exp(rel))
        o = jnp.einsum('bhts,bhse->bhte', att, vc) + jnp.einsum('bhtd,bhde->bhte', qc * jnp.exp(b), S)
        b_last = b[:, :, -1:, :]
        S_new = jnp.exp(b_last[:, :, 0, :])[..., None] * S + \
            jnp.einsum('bhsd,bhse->bhde', kc * jnp.exp(b_last - b), vc)
        return S_new, o

    S, o = lax.scan(step, S, tuple(to_chunks(a) for a in (q, k, v, log_a)))
    return S, from_chunks(o)


def bidirectional(scan_fn, ctx_shared, lat_shared, ctx_gates, lat_gates, init):
    outs_ctx, outs_lat = [], []
    for d in range(N_DIR):
        flip = (lambda a: jnp.flip(a, axis=2)) if d == 1 else (lambda a: a)
        state, h_ctx = scan_fn(*[flip(a) for a in ctx_shared + ctx_gates[d]], init)
        _, h_lat = scan_fn(*[flip(a) for a in lat_shared + lat_gates[d]], state)
        outs_ctx.append(flip(h_ctx))
        outs_lat.append(flip(h_lat))
    return outs_ctx[0] + outs_ctx[1], outs_lat[0] + outs_lat[1]


def mixer_streams(h, w_in, mlstm_conv, mlstm_gate_b, gla_w_gk2, gla_b_gk):
    bsz, t_len, _ = h.shape
    points = [int(p) for p in np.cumsum(IN_SIZES)[:-1]]
    qa, ka, va, oa, ga, ub, vb, qc, kc, vc, gc, gkc = jnp.split(h @ w_in, points, axis=-1)
    f32 = lambda a: a.astype(jnp.float32)
    qk = jax.nn.silu(centred_dwconv(jnp.concatenate([qa, ka], axis=-1), mlstm_conv))
    qa, ka = jnp.split(qk, 2, axis=-1)
    a_shared = (f32(split_heads(qa, A_HEADS)) * A_HEAD_DIM ** -0.5,
                f32(split_heads(ka, A_HEADS)), f32(split_heads(va, A_HEADS)))
    g = f32(ga).reshape(bsz, t_len, N_DIR, 2, A_HEADS) + f32(mlstm_gate_b)
    g = g.transpose(2, 3, 0, 4, 1)
    a_gates = tuple((g[d, 0], jax.nn.log_sigmoid(g[d, 1])) for d in range(N_DIR))
    c_shared = (f32(split_heads(qc, C_HEADS)) * C_HEAD_DK ** -0.5,
                f32(split_heads(kc, C_HEADS)), f32(split_heads(vc, C_HEADS)))
    gk = jnp.einsum('btdr,drk->dbtk', f32(gkc).reshape(bsz, t_len, N_DIR, C_GATE_RANK), f32(gla_w_gk2)) \
        + f32(gla_b_gk)[:, None, None, :]
    log_a = jax.nn.log_sigmoid(gk) / C_GATE_NORM
    c_gates = tuple((split_heads(log_a[d], C_HEADS),) for d in range(N_DIR))
    return a_shared, a_gates, c_shared, c_gates, (oa, gc, ub, vb)


def spatial_gating(ub, vb, norm_g, w_s, b_s):
    bsz, t_len, _ = vb.shape
    u = jax.nn.gelu(ub)
    v = rmsnorm(jax.nn.gelu(vb), norm_g).reshape(bsz, t_len // B_CHUNK, B_CHUNK, B_GROUPS, B_GROUP_DIM)
    mixed = jnp.einsum('gts,bnsgc->bntgc', w_s, v) + b_s.T[:, :, None]
    return u * mixed.reshape(bsz, t_len, B_WIDTH)


def mixer_output(a_h, c_h, oa, gc, ub, vb, mlstm_norm_g, gla_norm_g, sgu_norm_g, sgu_w, sgu_b, w_out):
    bsz, t_len, _ = oa.shape
    dt = oa.dtype
    a = rmsnorm(a_h.transpose(0, 2, 1, 3), mlstm_norm_g.reshape(A_HEADS, A_HEAD_DIM))
    a = a.reshape(bsz, t_len, A_WIDTH).astype(dt) * jax.nn.sigmoid(oa)
    cc = rmsnorm(c_h.transpose(0, 2, 1, 3), gla_norm_g).reshape(bsz, t_len, C_WIDTH).astype(dt) * jax.nn.silu(gc)
    s = spatial_gating(ub, vb, sgu_norm_g, sgu_w, sgu_b)
    return jnp.concatenate([a, s, cc], axis=-1) @ w_out


def mixer(h_ctx, h_lat, w_in, mlstm_conv, mlstm_gate_b, mlstm_norm_g, gla_w_gk2, gla_b_gk, gla_norm_g,
          sgu_norm_g, sgu_w, sgu_b, w_out, need_ctx):
    sc = mixer_streams(h_ctx, w_in, mlstm_conv, mlstm_gate_b, gla_w_gk2, gla_b_gk)
    sl = mixer_streams(h_lat, w_in, mlstm_conv, mlstm_gate_b, gla_w_gk2, gla_b_gk)
    bsz = h_lat.shape[0]
    a_init = (jnp.zeros((bsz, A_HEADS, A_HEAD_DIM, A_HEAD_DIM), jnp.float32),
              jnp.zeros((bsz, A_HEADS, A_HEAD_DIM), jnp.float32),
              jnp.zeros((bsz, A_HEADS), jnp.float32))
    c_init = jnp.zeros((bsz, C_HEADS, C_HEAD_DK, C_HEAD_DV), jnp.float32)
    a_ctx, a_lat = bidirectional(mlstm_scan, sc[0], sl[0], sc[1], sl[1], a_init)
    c_ctx, c_lat = bidirectional(gla_scan, sc[2], sl[2], sc[3], sl[3], c_init)
    o_lat = mixer_output(a_lat, c_lat, *sl[4], mlstm_norm_g, gla_norm_g, sgu_norm_g, sgu_w, sgu_b, w_out)
    o_ctx = mixer_output(a_ctx, c_ctx, *sc[4], mlstm_norm_g, gla_norm_g, sgu_norm_g, sgu_w, sgu_b, w_out) \
        if need_ctx else None
    return o_ctx, o_lat


def swiglu(h, w_in, w_out):
    gate, up = jnp.split(h @ w_in, 2, axis=-1)
    return (jax.nn.silu(gate) * up) @ w_out


def setup_inputs(seed: int = 0) -> dict:
    key = jax.random.key(seed)
    ks = jax.random.split(key, 24)
    nrm = lambda k, shape, scale: scale * jax.random.normal(k, shape, jnp.float32)
    gate_base = jnp.array([0.0, 3.0], jnp.float32)[None, :, None]
    return {
        'x': nrm(ks[0], (BATCH, SEQ, D_MODEL), 1.0),
        'c': nrm(ks[1], (BATCH, D_MODEL), 1.0),
        'ctx': nrm(ks[2], (BATCH, CTX_LEN, D_MODEL), 1.0),
        'c_ctx': nrm(ks[3], (D_MODEL,), 1.0),
        'norm1_g': 1.0 + nrm(ks[4], (DEPTH, D_MODEL), 0.02),
        'norm2_g': 1.0 + nrm(ks[5], (DEPTH, D_MODEL), 0.02),
        'w_ada': nrm(ks[6], (DEPTH, D_MODEL, 6 * D_MODEL), 0.5 * D_MODEL ** -0.5),
        'b_ada': nrm(ks[7], (DEPTH, 6 * D_MODEL), 0.02),
        'w_in': nrm(ks[8], (DEPTH, D_MODEL, N_IN), D_MODEL ** -0.5),
        'mlstm_conv': nrm(ks[9], (DEPTH, CONV_W, 2 * A_WIDTH), CONV_W ** -0.5),
        'mlstm_gate_b': gate_base + nrm(ks[10], (DEPTH, N_DIR, 2, A_HEADS), 0.3),
        'mlstm_norm_g': 1.0 + nrm(ks[11], (DEPTH, A_WIDTH), 0.02),
        'gla_w_gk2': nrm(ks[12], (DEPTH, N_DIR, C_GATE_RANK, C_KEY_WIDTH), C_GATE_RANK ** -0.5),
        'gla_b_gk': nrm(ks[13], (DEPTH, N_DIR, C_KEY_WIDTH), 0.1),
        'gla_norm_g': 1.0 + nrm(ks[14], (DEPTH, C_HEAD_DV), 0.02),
        'sgu_norm_g': 1.0 + nrm(ks[15], (DEPTH, B_WIDTH), 0.02),
        'sgu_w': nrm(ks[16], (DEPTH, B_GROUPS, B_CHUNK, B_CHUNK), B_CHUNK ** -0.5),
        'sgu_b': 1.0 + nrm(ks[17], (DEPTH, B_GROUPS, B_CHUNK), 0.02),
        'w_out': nrm(ks[18], (DEPTH, D_MODEL, D_MODEL), D_MODEL ** -0.5),
        'w_ffn_in': nrm(ks[19], (DEPTH, D_MODEL, 2 * D_FF), D_MODEL ** -0.5),
        'w_ffn_out': nrm(ks[20], (DEPTH, D_FF, D_MODEL), D_FF ** -0.5),
        'final_g': 1.0 + nrm(ks[21], (D_MODEL,), 0.02),
    }


def reference(x, c, ctx, c_ctx, norm1_g, norm2_g, w_ada, b_ada, w_in, mlstm_conv, mlstm_gate_b,
              mlstm_norm_g, gla_w_gk2, gla_b_gk, gla_norm_g, sgu_norm_g, sgu_w, sgu_b, w_out,
              w_ffn_in, w_ffn_out, final_g):
    rows = x.shape[1] // GRID_W
    x_lat, x_ctx = x, ctx
    s_lat = jax.nn.silu(c)
    s_ctx = jax.nn.silu(c_ctx)
    for l in range(DEPTH):
        need_ctx = l < DEPTH - 1
        mod_lat = (s_lat @ w_ada[l] + b_ada[l])[:, None, :]
        mod_ctx = (s_ctx @ w_ada[l] + b_ada[l])[None, None, :]
        sh1, sc1, g1, sh2, sc2, g2 = jnp.split(mod_lat, 6, axis=-1)
        csh1, csc1, cg1, csh2, csc2, cg2 = jnp.split(mod_ctx, 6, axis=-1)
        h_lat = modulate(rmsnorm(x_lat, norm1_g[l]), sh1, sc1)
        h_ctx = modulate(rmsnorm(x_ctx, norm1_g[l]), csh1, csc1)
        column_major = l % 2 == 1
        if column_major:
            h_lat = raster_to_column(h_lat, rows)
        o_ctx, o_lat = mixer(h_ctx, h_lat, w_in[l], mlstm_conv[l], mlstm_gate_b[l], mlstm_norm_g[l],
                             gla_w_gk2[l], gla_b_gk[l], gla_norm_g[l], sgu_norm_g[l], sgu_w[l], sgu_b[l],
                             w_out[l], need_ctx)
        if column_major:
            o_lat = column_to_raster(o_lat, rows)
        x_lat = x_lat + g1 * o_lat
        x_lat = x_lat + g2 * swiglu(modulate(rmsnorm(x_lat, norm2_g[l]), sh2, sc2), w_ffn_in[l], w_ffn_out[l])
        if need_ctx:
            x_ctx = x_ctx + cg1 * o_ctx
            x_ctx = x_ctx + cg2 * swiglu(modulate(rmsnorm(x_ctx, norm2_g[l]), csh2, csc2),
                                         w_ffn_in[l], w_ffn_out[l])
    return rmsnorm(x_lat, final_g)
```

```python
import contextlib
import numpy as np
import ml_dtypes
import concourse.bass as bass
import concourse.mybir as mybir
from concourse.bass_utils import run_bass_kernel_spmd

F32 = mybir.dt.float32
BF16 = mybir.dt.bfloat16
AF = mybir.ActivationFunctionType
ALU = mybir.AluOpType
AX = mybir.AxisListType

D = 1024
SEQ = 8192
CTX = 256
T = SEQ + CTX
NCH = T // 128
DEPTH = 4
NIN = 3248
DFF = 2816
EPS = 1e-6
ENGS = ("tensor", "vector", "scalar", "gpsimd", "sync")
NO_SELFWAIT = True


class Prog:
    def __init__(self, nc):
        self.nc = nc
        self.ops = {e: [] for e in ENGS}
        self.cnt = {}
        self.seen = {e: {} for e in ENGS}
        self.last_w = {}
        self.readers = {}
        self.waited = {}
        self.nops = 0
        self.nwaits = 0
        self.epoch = 0
        self.skip = False
        self.dma_waitmax = {}

    def _collect(self, eng, reads, writes):
        seen = self.seen[eng]
        best = {}

        def need(tok):
            k, v, clk = tok
            if eng == "tensor" and k.startswith("E_tensor"):
                return
            if seen.get(k, 0) >= v:
                return
            if k.startswith("D_"):
                v = self.cnt[k]
                clk = dict(clk)
                clk[k] = v
                if self.dma_waitmax.get(k, 0) < v:
                    self.dma_waitmax[k] = v
            if k not in best or best[k][0] < v:
                best[k] = (v, clk)

        for r in reads:
            w = self.last_w.get(r)
            if w is not None:
                need(w)
        for r in writes:
            w = self.last_w.get(r)
            if w is not None:
                need(w)
            for rd in self.readers.get(r, {}).values():
                need(rd)
        out = {}
        for k, (v, clk) in best.items():
            covered = False
            for k2, (v2, clk2) in best.items():
                if k2 != k and clk2.get(k, 0) >= v:
                    covered = True
                    break
            if not covered:
                out[k] = v
        for k, (v, clk) in best.items():
            for kk, vv in clk.items():
                if seen.get(kk, 0) < vv:
                    seen[kk] = vv
        return out

    def _commit(self, eng, reads, writes, semkey, val):
        clk = dict(self.seen[eng])
        clk[semkey] = val
        tok = (semkey, val, clk)
        for r in reads:
            self.readers.setdefault(r, {})[semkey] = tok
        for r in writes:
            self.last_w[r] = tok
            self.readers[r] = {}

    def _note(self, deps):
        for k, v in deps.items():
            if k.startswith("E_"):
                self.waited.setdefault(k, set()).add(v)

    def op(self, eng, fn, reads=(), writes=()):
        if self.skip:
            return
        deps = self._collect(eng, reads, writes)
        semkey = "E_%s_%d" % (eng, self.epoch)
        self.cnt[semkey] = self.cnt.get(semkey, 0) + 1
        val = self.cnt[semkey]
        self._note(deps)
        self.ops[eng].append((tuple(deps.items()), fn, semkey, val))
        self._commit(eng, reads, writes, semkey, val)
        self.nops += 1
        self.nwaits += len(deps)

    def dma(self, eng, fn, semkey, reads=(), writes=()):
        if self.skip:
            return
        semkey = "D_" + semkey
        deps = self._collect(eng, reads, writes)
        wm = self.dma_waitmax.get(semkey, 0)
        if NO_SELFWAIT is False and wm > self.seen[eng].get(semkey, 0) and wm > deps.get(semkey, 0):
            deps[semkey] = wm
            self.seen[eng][semkey] = wm
        self.cnt[semkey] = self.cnt.get(semkey, 0) + 16
        val = self.cnt[semkey]
        self._note(deps)
        self.ops[eng].append((tuple(deps.items()), fn, semkey, val))
        self._commit(eng, reads, writes, semkey, val)
        self.nops += 1
        self.nwaits += len(deps)

    def final_wait(self, eng, resources):
        deps = self._collect(eng, resources, ())
        self._note(deps)
        self.ops[eng].append((tuple(deps.items()), None, None, 0))

    def barrier(self, bump=False):
        allres = list(set(self.last_w.keys()) | set(self.readers.keys()))
        nc = self.nc
        self.op("gpsimd", lambda e: e.memset(self._bar_tile, 0.0), reads=(), writes=allres + ["_bar"])
        for eng in ENGS:
            if eng != "gpsimd":
                self.final_wait(eng, ["_bar"])
        if bump:
            self.epoch += 1

    def emit(self):
        nc = self.nc
        keys = sorted(self.cnt.keys())
        rank = {}
        for k, s in self.waited.items():
            rank[k] = {idx: i + 1 for i, idx in enumerate(sorted(s))}
        with contextlib.ExitStack() as st:
            sems = {}
            for k in keys:
                sems[k] = st.enter_context(nc.semaphore(k))
            block = st.enter_context(nc.Block())

            def run(e_obj, lst):
                for deps, fn, semkey, val in lst:
                    for k, v in deps:
                        if k.startswith("E_"):
                            e_obj.wait_ge(sems[k], rank[k][v])
                        else:
                            e_obj.wait_ge(sems[k], v)
                    if fn is None:
                        continue
                    ins = fn(e_obj)
                    if semkey.startswith("D_"):
                        ins.then_inc(sems[semkey], 16)
                    elif val in rank.get(semkey, ()):
                        ins.then_inc(sems[semkey], 1)

            @block.sync
            def _(e):
                run(e, self.ops["sync"])

            @block.tensor
            def _(e):
                run(e, self.ops["tensor"])

            @block.vector
            def _(e):
                run(e, self.ops["vector"])

            @block.scalar
            def _(e):
                run(e, self.ops["scalar"])

            @block.gpsimd
            def _(e):
                run(e, self.ops["gpsimd"])


class Arena:
    def __init__(self, nc, ncols):
        self.t = nc.alloc_sbuf_tensor("arena", [128, ncols], F32).ap()
        self.ncols = ncols
        self.top = 0

    def f32(self, cols):
        a = self.t[:, self.top:self.top + cols]
        self.top += cols
        assert self.top <= self.ncols, (self.top, self.ncols)
        return a

    def bf16(self, cols):
        c32 = (cols + 1) // 2
        return self.f32(c32).bitcast(BF16)[:, 0:cols]

    def mark(self):
        return self.top

    def release(self, m):
        self.top = m


def build_program(depth=DEPTH, dbg=(), stop_after=None, b_limit=None, b_parts="mg"):
    nc = bass.Bass("TRN2", target_bir_lowering=False)
    P = Prog(nc)

    def din(name, shape, dt=F32):
        return nc.dram_tensor(name, list(shape), dt, kind="ExternalInput").ap()

    def dscr(name, shape, dt=F32):
        kind = "ExternalOutput" if name in dbg else "Internal"
        return nc.dram_tensor(name, list(shape), dt, kind=kind).ap()

    L = depth
    x_d = din("x", [SEQ, D]); ctx_d = din("ctx", [CTX, D]); s_in_d = din("s_in", [128, 16])
    wada_d = din("w_ada", [L, D, 6 * D]); bada_fm_d = din("b_ada_fm", [128, L * 48]); bada_d = din("b_ada", [L, 6 * D])
    n1_d = din("norm1_fm", [128, L * 8]); n2_d = din("norm2_fm", [128, L * 8])
    win_d = din("w_in", [L, D, NIN]); wout_d = din("w_out", [L, D, D])
    wfi_d = din("w_ffn_in", [L, D, 2 * DFF]); wfo_d = din("w_ffn_out", [L, DFF, D])
    conv_d = din("conv_fm", [96, L * 24]); gateb_d = din("gate_b", [L, 16])
    mng_d = din("mlstm_norm_g", [L, 384]); gng_d = din("gla_norm_g", [L, 96]); sng_d = din("sgu_norm_g", [L, 256])
    wgk_d = din("wgk2_pad", [16, L * 4 * 128]); bgk_d = din("bgk_pad", [128, L * 4])
    swT_d = din("sgu_wT", [128, L * 4 * 128]); sbT_d = din("sgu_bT", [128, L * 4])
    fg_d = din("final_g", [1, D])
    ident_d = din("ident", [128, 128]); maskF_d = din("maskF", [128, 128]); maskB_d = din("maskB", [128, 128])
    reset_d = din("resetmask", [128, 512])
    y_d = nc.dram_tensor("y", [SEQ, D], F32, kind="ExternalOutput").ap()

    xs_d = dscr("xs", [SEQ, D]); xc_d = dscr("xc", [CTX, D])
    QK_d = dscr("QK", [8, 96, T]); CQK_d = dscr("CQK", [4, 128, T]); GKC_d = dscr("GKC", [2, 16, T])
    TM_d = dscr("TM", [T, 2064])
    QKc_d = dscr("QKc", [8, 96, T], BF16); KTM_d = dscr("KTM", [T, 384], BF16)
    GQ_d = dscr("GQ", [2, 2, 128, T], BF16); GK_d = dscr("GK", [2, 2, 128, T], BF16)
    GKT_d = dscr("GKT", [2, T, 256], BF16); DEC_d = dscr("DEC", [2, 2, 128, NCH])
    HA_d = dscr("HA", [2, T, 384]); HC_d = dscr("HC", [2, T, 384])
    CAT_d = dscr("CAT", [T, 1024]) if "CAT" in dbg else None

    ar = Arena(nc, 53000)
    P._bar_tile = nc.alloc_sbuf_tensor("bar_t", [128, 8], F32).ap()
    ps = [nc.alloc_psum_tensor("psb%d" % i, [128, 512], F32).ap() for i in range(8)]
    PS = lambda i: "ps%d" % i

    CUR = [0]
    HEAVY = ("bl0", "bl1", "cl0", "cl1", "a2s2", "a2g")

    def kx(key):
        return key + "_%d" % CUR[0] if key in HEAVY else key

    def ld(out, in_, key, W, R=(), slow=False):
        if slow:
            P.dma("sync", lambda e: e.dma_start(out=out, in_=in_, allow_slow_non_contiguous=True), kx(key), reads=list(R), writes=list(W))
        else:
            P.dma("sync", lambda e: e.dma_start(out=out, in_=in_), kx(key), reads=list(R), writes=list(W))

    def stq(out, in_, key, R, W):
        P.dma("gpsimd", lambda e: e.dma_start(out=out, in_=in_), kx(key), reads=list(R), writes=list(W))

    def mm(out, lhsT, rhs, start, stop, R, W):
        P.op("tensor", lambda e: e.matmul(out, lhsT=lhsT, rhs=rhs, start=start, stop=stop), reads=list(R), writes=list(W))

    def tr(out, in_, ident, R, W):
        P.op("tensor", lambda e: e.transpose(out=out, in_=in_, identity=ident), reads=list(R), writes=list(W))

    def act(out, in_, func, R, W, scale=None, bias=None, accum=None):
        kw = {}
        if scale is not None:
            kw["scale"] = scale
        if bias is not None:
            kw["bias"] = bias
        if accum is not None:
            kw["accum_out"] = accum
        P.op("scalar", lambda e: e.activation(out=out, in_=in_, func=func, **kw), reads=list(R), writes=list(W))

    def ts(out, in0, s1, s2, op0, op1, R, W, eng="vector"):
        if op1 is None:
            P.op(eng, lambda e: e.tensor_scalar(out=out, in0=in0, scalar1=s1, scalar2=0.0, op0=op0, op1=ALU.add), reads=list(R), writes=list(W))
        else:
            P.op(eng, lambda e: e.tensor_scalar(out=out, in0=in0, scalar1=s1, scalar2=s2, op0=op0, op1=op1), reads=list(R), writes=list(W))

    def tt(out, in0, in1, op, R, W, eng="vector"):
        P.op(eng, lambda e: e.tensor_tensor(out=out, in0=in0, in1=in1, op=op), reads=list(R), writes=list(W))

    def stt(out, in0, scalar, in1, op0, op1, R, W):
        P.op("vector", lambda e: e.scalar_tensor_tensor(out=out, in0=in0, scalar=scalar, in1=in1, op0=op0, op1=op1), reads=list(R), writes=list(W))

    def cp(out, in_, R, W, eng="vector"):
        if eng == "scalar":
            P.op("scalar", lambda e: e.copy(out=out, in_=in_), reads=list(R), writes=list(W))
        else:
            P.op(eng, lambda e: e.tensor_copy(out=out, in_=in_), reads=list(R), writes=list(W))

    def recip(out, in_, R, W):
        P.op("vector", lambda e: e.reciprocal(out=out, in_=in_), reads=list(R), writes=list(W))

    def rsum(out, in_, R, W):
        P.op("vector", lambda e: e.tensor_reduce(out=out, in_=in_, axis=AX.X, op=ALU.add), reads=list(R), writes=list(W))

    def scan(out, d0, d1, R, W):
        P.op("vector", lambda e: e.tensor_tensor_scan(out=out, data0=d0, data1=d1, initial=0.0, op0=ALU.mult, op1=ALU.add), reads=list(R), writes=list(W))

    def mset(ap, val, W, eng="gpsimd"):
        P.op(eng, lambda e: e.memset(ap, val), reads=(), writes=list(W))

    def rstd_from_ss(rstd, ss, n, R, W):
        ts(rstd, ss, 1.0 / n, EPS, ALU.mult, ALU.add, R, W)
        act(rstd, rstd, AF.Sqrt, W, W)
        recip(rstd, rstd, W, W)

    ident = ar.f32(128); maskF = ar.f32(128); maskB = ar.f32(128); ones = ar.f32(128)
    S_fm = ar.f32(16)
    modv = ar.f32(4 * 16)
    Gbc = ar.f32(4 * 1024)
    n1 = ar.f32(L * 8); n2 = ar.f32(L * 8); badafm = ar.f32(L * 48)
    ld(ident, ident_d, "c1", ["ident"]); ld(maskF, maskF_d, "c2", ["maskF"]); ld(maskB, maskB_d, "c3", ["maskB"])
    ld(S_fm, s_in_d, "c4", ["S_fm"])
    ld(n1, n1_d, "c5", ["n1"]); ld(n2, n2_d, "c0", ["n2"]); ld(badafm, bada_fm_d, "c6", ["badafm"])
    mset(ones, 1.0, ["ones"])
    act(S_fm, S_fm, AF.Silu, ["S_fm"], ["S_fm"])
    modv3 = modv.rearrange("p (q m v) -> p q m v", q=4, m=8)
    base_top = ar.mark()

    def lat_rows(l, j):
        if l % 2 == 0:
            return lambda t: t[j * 128:(j + 1) * 128, :]
        return lambda t: t.rearrange("(r w) d -> w r d", w=64)[j]

    def tile_src(l, c):
        if c < 2:
            src = ctx_d if l == 0 else xc_d
            return src[c * 128:(c + 1) * 128, :], ("xc", c)
        j = c - 2
        src = x_d if l == 0 else xs_d
        return lat_rows(l, j)(src), ("xs", l % 2, j)

    def tile_dst(l, c, final=False):
        if c < 2:
            return xc_d[c * 128:(c + 1) * 128, :], ("xc", c)
        j = c - 2
        if final:
            return lat_rows(l, j)(y_d), ("y", j)
        return lat_rows(l, j)(xs_d), ("xs", l % 2, j)

    def load_cast(w_rows, ncols, dest3, stage, stage_res, dest_res, key, col0=0, kchunks=8, dcol0=0):
        i = 0
        for kc in range(kchunks):
            c = 0
            while c < ncols:
                n = min(2816, ncols - c)
                s = i % 2
                ld(stage[s][:, 0:n], w_rows(kc)[:, col0 + c:col0 + c + n], key + str(s), [stage_res + str(s)])
                eng = ("vector", "gpsimd", "scalar")[i % 3]
                cp(dest3[:, kc, dcol0 + c:dcol0 + c + n], stage[s][:, 0:n], [stage_res + str(s)], [dest_res], eng=eng)
                c += n
                i += 1

    for l in range(L):
        last = (l == L - 1)
        CUR[0] = l
        P.barrier(bump=True)
        ar.release(base_top)
        m0 = ar.mark()
        wa = ar.f32(8 * 1024)
        wa3 = wa.rearrange("p (k c) -> p k c", k=8)
        brow = ar.f32(1024)
        Sb = ar.f32(16 * 128)
        Sb3 = Sb.rearrange("p (a b) -> p a b", a=16)
        cp(Sb3, S_fm.unsqueeze(2).to_broadcast([128, 16, 128]), ["S_fm"], ["Sb"])
        for qi, piece in enumerate((0, 1, 3, 4, 2, 5)):
            for kc in range(8):
                ld(wa3[:, kc, :], wada_d[l, kc * 128:(kc + 1) * 128, piece * 1024:(piece + 1) * 1024], "wa", ["wa"])
            if qi < 4:
                for mc in range(8):
                    for kc in range(8):
                        mm(ps[0][:, mc * 2:mc * 2 + 2], wa3[:, kc, mc * 128:(mc + 1) * 128], S_fm[:, kc * 2:kc * 2 + 2],
                           kc == 0, kc == 7, ["wa", "S_fm"], [PS(0)])
                bsl = badafm[:, l * 48 + piece * 8: l * 48 + piece * 8 + 8].unsqueeze(2).to_broadcast([128, 8, 2])
                tt(modv3[:, qi], ps[0][:, 0:16].rearrange("p (m v) -> p m v", v=2), bsl, ALU.add, ["badafm"], [PS(0), "modv"])
                if qi in (1, 3):
                    ng = (n1 if qi == 1 else n2)[:, l * 8:(l + 1) * 8].unsqueeze(2).to_broadcast([128, 8, 2])
                    ts(modv3[:, qi], modv3[:, qi], 1.0, None, ALU.add, None, ["modv"], ["modv"])
                    tt(modv3[:, qi], modv3[:, qi], ng, ALU.mult, ["modv", "n1", "n2"], ["modv"])
            else:
                gi = qi - 4
                ld(brow, bada_d[l:l + 1, piece * 1024:(piece + 1) * 1024].partition_broadcast(128), "wa2", ["brow"])
                for v in range(2):
                    for half in range(2):
                        for kc in range(8):
                            mm(ps[1 + half][:, :], Sb3[:, kc * 2 + v, :], wa3[:, kc, half * 512:(half + 1) * 512],
                               kc == 0, kc == 7, ["wa", "Sb"], [PS(1 + half)])
                        gsl = Gbc[:, (gi * 2 + v) * 1024 + half * 512:(gi * 2 + v) * 1024 + (half + 1) * 512]
                        tt(gsl, ps[1 + half][:, :], brow[:, half * 512:(half + 1) * 512], ALU.add, ["brow"], [PS(1 + half), "Gbc"])
        ar.release(m0)
        P.barrier()
        if stop_after == "0":
            break

        def modsc(q, kc, v):
            return modv3[:, q, kc, v:v + 1]

        def norm_to_hT(xt, xres, hT3, col0, qsh, qmu, v, sfx):
            ss = small[:, 0:1]; rstd = small[:, 1:2]
            act(xn, xt, AF.Square, [xres], ["xn", "small"], accum=ss)
            rstd_from_ss(rstd, ss, D, ["small"], ["small"])
            act(xn, xt, AF.Identity, [xres, "small"], ["xn"], scale=rstd)
            for half in range(2):
                for j in range(4):
                    kc = half * 4 + j
                    tr(ps[half][:, j * 128:(j + 1) * 128], xn[:, kc * 128:(kc + 1) * 128], ident, ["xn", "ident"], [PS(half)])
                for j in range(4):
                    kc = half * 4 + j
                    if j % 2 == 0:
                        act(hT3[:, kc, col0:col0 + 128], ps[half][:, j * 128:(j + 1) * 128], AF.Identity, ["modv"], [PS(half), "hT" + sfx],
                            scale=modsc(qmu, kc, v), bias=modsc(qsh, kc, v))
                    else:
                        ts(hT3[:, kc, col0:col0 + 128], ps[half][:, j * 128:(j + 1) * 128], modsc(qmu, kc, v), modsc(qsh, kc, v),
                           ALU.mult, ALU.add, ["modv"], [PS(half), "hT" + sfx])

        mA = ar.mark()
        winb = ar.bf16(8 * NIN); winb3 = winb.rearrange("p (k c) -> p k c", k=8)
        wpad = ar.bf16(8 * 512); wpad3 = wpad.rearrange("p (k c) -> p k c", k=8)
        stage = [ar.f32(2816), ar.f32(2816)]
        xbuf = [ar.f32(1024), ar.f32(1024)]
        xn = ar.f32(1024); small = ar.f32(16)
        hTb = ar.bf16(8 * 512); hT3 = hTb.rearrange("p (k c) -> p k c", k=8)
        fmst = [ar.f32(512), ar.f32(512)]
        tmst = [ar.f32(2064), ar.f32(2064)]
        load_cast(lambda kc: win_d[l, kc * 128:(kc + 1) * 128, :], NIN, winb3, stage, "stg", "winb", "wl")
        P.op("vector", lambda e: e.memset(wpad, 0.0), reads=(), writes=["wpad"])
        for gq in range(4):
            for hh in range(2):
                c0 = 2064 + (gq // 2) * 192 + ((gq % 2) * 2 + hh) * 48
                cp(wpad3[:, :, gq * 128 + hh * 64: gq * 128 + hh * 64 + 48], winb3[:, :, c0:c0 + 48], ["winb"], ["wpad"], eng="gpsimd")
        segs = []
        c = 0
        while c < NCH:
            n = 2 if c < 2 else 4
            segs.append((c, n))
            c += n
        fm_i = 0
        tm_i = 0
        for (c0, nt) in segs:
            N = nt * 128
            v = 1 if c0 < 2 else 0
            for i in range(nt):
                src, sres = tile_src(l, c0 + i)
                xt = xbuf[(c0 + i) % 2]; xres = "xbuf%d" % ((c0 + i) % 2)
                ld(xt, src, "xl" + str((c0 + i) % 2), [xres], R=[sres])
                norm_to_hT(xt, xres, hT3, i * 128, 0, 1, v, "")
            tau0 = c0 * 128
            groups = []
            for g in range(8):
                groups.append((winb3, g * 96, 96, QK_d[g, :, tau0:tau0 + N], ("QK", c0), None))
            for gq in range(4):
                groups.append((wpad3, gq * 128, 128, CQK_d[gq, :, tau0:tau0 + N], ("CQK", c0), (48.0 ** -0.5) if gq < 2 else None))
            for dd in range(2):
                groups.append((winb3, 3216 + dd * 16, 16, GKC_d[dd, :, tau0:tau0 + N], ("GKC", c0), None))
            for (w3, col, M, dst, dres, scl) in groups:
                b = 2 + fm_i % 2
                st_t = fmst[fm_i % 2]; sres2 = "fmst%d" % (fm_i % 2)
                for kc in range(8):
                    mm(ps[b][0:M, 0:N], w3[:, kc, col:col + M], hT3[:, kc, 0:N], kc == 0, kc == 7, ["winb", "wpad", "hT"], [PS(b)])
                if scl is not None:
                    act(st_t[0:M, 0:N], ps[b][0:M, 0:N], AF.Identity, [], [PS(b), sres2], scale=scl)
                elif fm_i % 2 == 0:
                    cp(st_t[0:M, 0:N], ps[b][0:M, 0:N], [], [PS(b), sres2], eng="vector")
                else:
                    cp(st_t[0:M, 0:N], ps[b][0:M, 0:N], [], [PS(b), sres2], eng="scalar")
                stq(dst, st_t[0:M, 0:N], "fs" + str(fm_i % 2), [sres2], [dres])
                fm_i += 1
            tmsegs = [(768, 512, 0), (1280, 256, 512), (1552, 512, 768), (2448, 512, 1280), (2960, 256, 1792), (1536, 16, 2048)]
            for i in range(nt):
                tmt = tmst[tm_i % 2]; tres = "tmst%d" % (tm_i % 2)
                for si, (wc, n, oc) in enumerate(tmsegs):
                    b = 4 + si % 4
                    for kc in range(8):
                        mm(ps[b][:, 0:n], hT3[:, kc, i * 128:(i + 1) * 128], winb3[:, kc, wc:wc + n], kc == 0, kc == 7, ["winb", "hT"], [PS(b)])
                    cp(tmt[:, oc:oc + n], ps[b][:, 0:n], [], [PS(b), tres], eng=("vector" if si % 2 == 0 else "scalar"))
                stq(TM_d[tau0 + i * 128: tau0 + (i + 1) * 128, :], tmt, "ts" + str(tm_i % 2), [tres], [("TM", c0 + i)])
                tm_i += 1
        ar.release(mA)
        P.barrier()
        if stop_after == "A":
            break

        mA2 = ar.mark()
        cw = ar.f32(24); cw3 = cw.rearrange("p (g j) -> p g j", j=3)
        ld(cw[0:96, :], conv_d[:, l * 24:(l + 1) * 24], "c1", ["cw"])
        wgk = ar.f32(512); wgk3 = wgk.rearrange("p (a c) -> p a c", a=4)
        ld(wgk[0:16, :], wgk_d[:, l * 512:(l + 1) * 512], "c2", ["wgk"])
        negb = ar.f32(4)
        resetm = ar.f32(512)
        ld(resetm, reset_d, "c4", ["resetm"])
        ld(negb, bgk_d[:, l * 4:(l + 1) * 4], "c3", ["negb"])
        ts(negb, negb, -1.0, None, ALU.mult, None, ["negb"], ["negb"])
        X = ar.f32(8 * 514); X3 = X.rearrange("p (g t) -> p g t", g=8)
        Y0 = ar.f32(8 * 512); Y03 = Y0.rearrange("p (g t) -> p g t", g=8)
        Y1 = ar.f32(8 * 512); Y13 = Y1.rearrange("p (g t) -> p g t", g=8)
        qkb = ar.bf16(8 * 512); qkb3 = qkb.rearrange("p (g t) -> p g t", g=8)
        ktm = ar.bf16(4 * 384); ktm3 = ktm.rearrange("p (c e) -> p c e", c=4)
        gkc_t = ar.f32(512); qc_t = ar.f32(512); kc_t = ar.f32(512)
        e_t = ar.f32(512); sp_t = ar.f32(512); c_t = ar.f32(512); cc_t = ar.f32(512)
        eb_t = ar.f32(512); enb_t = ar.f32(512); k32 = ar.f32(512)
        qtb = ar.bf16(512); ktb = ar.bf16(512); gktb = ar.bf16(512)
        dec_t = ar.f32(4)
        QKr = QK_d.rearrange("g d t -> d g t")
        QKcr = QKc_d.rearrange("g d t -> d g t")
        for (c0, nt) in segs:
            N = nt * 128
            tau0 = c0 * 128
            ld(X3[0:96, :, 1:N + 1], QKr[:, :, tau0:tau0 + N], "a2x", ["X"], R=[("QK", c0)])
            first = (c0 == 0 or c0 == 2)
            lastseg = (c0 + nt == 2 or c0 + nt == NCH)
            if first:
                P.op("vector", lambda e: e.memset(X3[0:96, :, 0:1], 0.0), reads=(), writes=["X"])
            else:
                ld(X3[0:96, :, 0:1], QKr[:, :, tau0 - 1:tau0], "a2x", ["X"], R=[("QK", c0 - 4 if c0 > 2 else 0)], slow=True)
            if lastseg:
                P.op("vector", lambda e, N=N: e.memset(X3[0:96, :, N + 1:N + 2], 0.0), reads=(), writes=["X"])
            else:
                ld(X3[0:96, :, N + 1:N + 2], QKr[:, :, tau0 + N:tau0 + N + 1], "a2x", ["X"], R=[("QK", c0 + nt)], slow=True)
            wb = lambda j: cw3[0:96, :, j:j + 1].to_broadcast([96, 8, N])
            tt(Y03[0:96, :, 0:N], X3[0:96, :, 0:N], wb(0), ALU.mult, ["X", "cw"], ["Y0"], eng="gpsimd")
            tt(Y13[0:96, :, 0:N], X3[0:96, :, 1:N + 1], wb(1), ALU.mult, ["X", "cw"], ["Y1"])
            tt(Y03[0:96, :, 0:N], Y03[0:96, :, 0:N], Y13[0:96, :, 0:N], ALU.add, ["Y0", "Y1"], ["Y0"])
            tt(Y13[0:96, :, 0:N], X3[0:96, :, 2:N + 2], wb(2), ALU.mult, ["X", "cw"], ["Y1"], eng="gpsimd")
            tt(Y03[0:96, :, 0:N], Y03[0:96, :, 0:N], Y13[0:96, :, 0:N], ALU.add, ["Y0", "Y1"], ["Y0"])
            act(Y03[0:96, :, 0:N], Y03[0:96, :, 0:N], AF.Silu, ["Y0"], ["Y0"])
            ts(qkb3[0:96, 0:4, 0:N], Y03[0:96, 0:4, 0:N], 96.0 ** -0.5, None, ALU.mult, None, ["Y0"], ["qkb"])
            cp(qkb3[0:96, 4:8, 0:N], Y03[0:96, 4:8, 0:N], ["Y0"], ["qkb"], eng="gpsimd")
            stq(QKcr[:, :, tau0:tau0 + N], qkb3[0:96, :, 0:N], "a2s", ["qkb"], [("QKc", c0)])
            for ci in range(nt):
                b = ci % 2
                for h in range(4):
                    tr(ps[b][:, h * 96:(h + 1) * 96], Y03[0:96, 4 + h, ci * 128:(ci + 1) * 128], ident[0:96, 0:96], ["Y0", "ident"], [PS(b)])
                cp(ktm3[:, ci, :], ps[b][:, 0:384], [], [PS(b), "ktm"], eng=("vector" if ci % 2 == 0 else "scalar"))
            stq(KTM_d[tau0:tau0 + N, :].rearrange("(c p) e -> p c e", p=128), ktm3[:, 0:nt, :], "a2s", ["ktm"], [("KTM", c0)])
            for dd in range(2):
                ld(gkc_t[0:16, 0:N], GKC_d[dd, :, tau0:tau0 + N], "a2g", ["gkc_t"], R=[("GKC", c0)])
                for g in range(2):
                    ld(qc_t[:, 0:N], CQK_d[g, :, tau0:tau0 + N], "a2g", ["qc_t"], R=[("CQK", c0)])
                    ld(kc_t[:, 0:N], CQK_d[2 + g, :, tau0:tau0 + N], "a2g", ["kc_t"], R=[("CQK", c0)])
                    mm(ps[4][:, 0:N], wgk3[0:16, dd * 2 + g, :], gkc_t[0:16, 0:N], True, True, ["wgk", "gkc_t"], [PS(4)])
                    act(e_t[:, 0:N], ps[4][:, 0:N], AF.Exp, ["negb"], [PS(4), "e_t"], scale=-1.0, bias=negb[:, dd * 2 + g:dd * 2 + g + 1])
                    act(sp_t[:, 0:N], e_t[:, 0:N], AF.Ln, ["e_t"], ["sp_t"], bias=1.0)
                    scan(c_t[:, 0:N], resetm[:, 0:N], sp_t[:, 0:N], ["resetm", "sp_t"], ["c_t"])
                    c3 = c_t[:, 0:N].rearrange("p (c t) -> p c t", t=128)
                    if dd == 0:
                        ccur = c_t; cres = "c_t"
                    else:
                        cc3 = cc_t[:, 0:N].rearrange("p (c t) -> p c t", t=128)
                        tt(cc3, c3[:, :, 127:128].to_broadcast([128, nt, 128]), c3, ALU.subtract, ["c_t"], ["cc_t"])
                        tt(cc_t[:, 0:N], cc_t[:, 0:N], sp_t[:, 0:N], ALU.add, ["cc_t", "sp_t"], ["cc_t"])
                        ccur = cc_t; cres = "cc_t"
                    act(eb_t[:, 0:N], ccur[:, 0:N], AF.Exp, [cres], ["eb_t"], scale=-1.0 / 16)
                    act(enb_t[:, 0:N], ccur[:, 0:N], AF.Exp, [cres], ["enb_t"], scale=1.0 / 16)
                    tt(qtb[:, 0:N], qc_t[:, 0:N], eb_t[:, 0:N], ALU.mult, ["qc_t", "eb_t"], ["qtb"])
                    tt(k32[:, 0:N], kc_t[:, 0:N], enb_t[:, 0:N], ALU.mult, ["kc_t", "enb_t"], ["k32"], eng="gpsimd")
                    cp(ktb[:, 0:N], k32[:, 0:N], ["k32"], ["ktb"], eng="gpsimd")
                    eb3 = eb_t[:, 0:N].rearrange("p (c t) -> p c t", t=128)
                    tpos = 127 if dd == 0 else 0
                    cp(dec_t[:, 0:nt].unsqueeze(2), eb3[:, :, tpos:tpos + 1], ["eb_t"], ["dec_t"])
                    stq(DEC_d[dd, g, :, c0:c0 + nt], dec_t[:, 0:nt], "a2s2", ["dec_t"], [("DEC", dd, c0)])
                    stq(GQ_d[dd, g, :, tau0:tau0 + N], qtb[:, 0:N], "a2s2", ["qtb"], [("GQ", dd, c0)])
                    stq(GK_d[dd, g, :, tau0:tau0 + N], ktb[:, 0:N], "a2s2", ["ktb"], [("GK", dd, c0)])
                    for ci in range(nt):
                        tr(ps[5][:, ci * 128:(ci + 1) * 128], k32[:, ci * 128:(ci + 1) * 128], ident, ["k32", "ident"], [PS(5)])
                    cp(gktb[:, 0:N], ps[5][:, 0:N], [], [PS(5), "gktb"], eng="scalar")
                    stq(GKT_d[dd, tau0:tau0 + N, g * 128:(g + 1) * 128].rearrange("(c p) e -> p c e", p=128),
                        gktb[:, 0:N].rearrange("p (c e) -> p c e", e=128), "a2s2", ["gktb"], [("GKT", dd, c0)])
        ar.release(mA2)
        P.barrier()
        if stop_after == "A2":
            break

        mB = ar.mark()
        gb = ar.f32(16)
        ld(gb, gateb_d[l:l + 1, :].partition_broadcast(128), "c1", ["gb"])
        S32 = [ar.f32(4 * 97), ar.f32(4 * 97)]
        Sbf = [ar.bf16(4 * 97), ar.bf16(4 * 97)]
        G32 = [ar.f32(4 * 96), ar.f32(4 * 96)]
        Gbf = [ar.bf16(4 * 96), ar.bf16(4 * 96)]
        NS = 2
        qk_b = [ar.bf16(8 * 128) for _ in range(NS)]
        ktm_b = [ar.bf16(384) for _ in range(NS)]
        va_t = [ar.f32(384) for _ in range(NS)]
        ga_t = [ar.f32(16) for _ in range(NS)]
        gq_b = [ar.bf16(512) for _ in range(NS)]
        gk_b = [ar.bf16(512) for _ in range(NS)]
        gkt_b = [ar.bf16(256) for _ in range(NS)]
        vc_t = [ar.f32(384) for _ in range(NS)]
        dec_b = [ar.f32(4) for _ in range(NS)]
        gg = ar.f32(8); spf = ar.f32(4); tmpg = ar.f32(4); beta = ar.f32(4); flo = ar.f32(4); dcy = ar.f32(4)
        den = ar.f32(4); rc = ar.f32(4)
        vw = ar.bf16(4 * 97); vw3 = vw.rearrange("p (h e) -> p h e", h=4)
        ATb = ar.bf16(512); AT3 = ATb.rearrange("p (h t) -> p h t", h=4)
        ATg = ar.bf16(512); ATg3 = ATg.rearrange("p (h t) -> p h t", h=4)
        hout = [ar.f32(384), ar.f32(384)]
        gout = [ar.f32(384), ar.f32(384)]
        vcb = ar.bf16(384)
        stmp = ar.f32(4 * 97); gtmp = ar.f32(4 * 96)
        for dd in range(2):
            order = list(range(NCH)) if dd == 0 else [1, 0] + list(range(NCH - 1, 1, -1))
            mask = maskF if dd == 0 else maskB
            mres = "maskF" if dd == 0 else "maskB"
            S32d = S32[dd]; Sbfd = Sbf[dd]; G32d = G32[dd]; Gbfd = Gbf[dd]
            sr = "S%d" % dd
            mset(S32d, 0.0, [sr + "32"]); mset(Sbfd, 0.0, [sr + "bf"]); mset(G32d, 0.0, [sr + "g32"]); mset(Gbfd, 0.0, [sr + "gbf"])
            S323 = S32d.rearrange("p (h e) -> p h e", h=4); Sbf3 = Sbfd.rearrange("p (h e) -> p h e", h=4)
            G323 = G32d.rearrange("p (g e) -> p g e", g=4); Gbf3 = Gbfd.rearrange("p (g e) -> p g e", g=4)

            def loads(idx):
                c = order[idx]
                s = idx % NS
                tau0 = c * 128
                sg = (c // 4) * 4 if c >= 2 else 0
                sgc = 0 if c < 2 else 2 + ((c - 2) // 4) * 4
                ld(qk_b[s][0:96, :].rearrange("p (g t) -> p g t", g=8), QKcr[:, :, tau0:tau0 + 128], "bl%d" % s, ["qk_b%d" % s], R=[("QKc", sgc)])
                ld(ktm_b[s], KTM_d[tau0:tau0 + 128, :], "bl%d" % s, ["ktm_b%d" % s], R=[("KTM", sgc)])
                ld(va_t[s], TM_d[tau0:tau0 + 128, 0:384], "bl%d" % s, ["va_t%d" % s], R=[("TM", c)])
                ld(ga_t[s], TM_d[tau0:tau0 + 128, 2048:2064], "bl%d" % s, ["ga_t%d" % s], R=[("TM", c)])
                ld(gq_b[s][0:64, :].rearrange("p (h t) -> p h t", h=4), GQ_d[dd, :, :, tau0:tau0 + 128].rearrange("g (hh d) t -> d (g hh) t", hh=2), "bl%d" % s, ["gq_b%d" % s], R=[("GQ", dd, sgc)])
                ld(gk_b[s][0:64, :].rearrange("p (h t) -> p h t", h=4), GK_d[dd, :, :, tau0:tau0 + 128].rearrange("g (hh d) t -> d (g hh) t", hh=2), "bl%d" % s, ["gk_b%d" % s], R=[("GK", dd, sgc)])
                ld(gkt_b[s], GKT_d[dd, tau0:tau0 + 128, :], "bl%d" % s, ["gkt_b%d" % s], R=[("GKT", dd, sgc)])
                ld(vc_t[s], TM_d[tau0:tau0 + 128, 1280:1664], "bl%d" % s, ["vc_t%d" % s], R=[("TM", c)])
                ld(dec_b[s][0:64, :].rearrange("p (g o) -> p g o", o=1), DEC_d[dd, :, :, c:c + 1].rearrange("g (hh d) o -> d (g hh) o", hh=2), "bl%d" % s, ["dec_b%d" % s], R=[("DEC", dd, sgc)], slow=True)

            loads(0)
            for idx in range(NCH if b_limit is None else b_limit):
                c = order[idx]
                s = idx % NS
                tau0 = c * 128
                if idx + 1 < NCH:
                    loads(idx + 1)
                qk3 = qk_b[s].rearrange("p (g t) -> p g t", g=8)
                rs = lambda nm: nm + str(s)
                P.skip = "m" not in b_parts
                tt(gg, ga_t[s][:, dd * 8:(dd + 1) * 8], gb[:, dd * 8:(dd + 1) * 8], ALU.add, [rs("ga_t"), "gb"], ["gg"])
                act(spf, gg[:, 4:8], AF.Exp, ["gg"], ["spf"], scale=-1.0)
                act(spf, spf, AF.Ln, ["spf"], ["spf"], bias=1.0)
                mm(ps[0][:, 0:4], mask, spf, True, True, [mres, "spf"], [PS(0)])
                mm(ps[0][0:96, 8:12], ones[:, 0:96], spf, True, True, ["ones", "spf"], [PS(0)])
                tt(tmpg, gg[:, 0:4], ps[0][:, 0:4], ALU.add, ["gg"], [PS(0), "tmpg"])
                act(beta, tmpg, AF.Exp, ["tmpg"], ["beta"])
                act(flo, ps[0][:, 0:4], AF.Exp, [], [PS(0), "flo"])
                act(dcy[0:96, :], ps[0][0:96, 8:12], AF.Exp, [], [PS(0), "dcy"], scale=-1.0)
                va3 = va_t[s].rearrange("p (h e) -> p h e", h=4)
                tt(vw3[:, :, 0:96], va3, beta.unsqueeze(2).to_broadcast([128, 4, 96]), ALU.mult, [rs("va_t"), "beta"], ["vw"])
                cp(vw3[:, :, 96:97], beta.unsqueeze(2), ["beta"], ["vw"], eng="gpsimd")
                for h in range(4):
                    mm(ps[1][:, h * 128:(h + 1) * 128], qk3[0:96, 4 + h, :], qk3[0:96, h, :], True, True, [rs("qk_b")], [PS(1)])
                tt(AT3, ps[1].rearrange("p (h t) -> p h t", h=4), mask.unsqueeze(1).to_broadcast([128, 4, 128]), ALU.mult, [mres], [PS(1), "ATb"])
                for h in range(4):
                    mm(ps[2][:, h * 97:(h + 1) * 97], AT3[:, h, :], vw3[:, h, :], True, False, ["ATb", "vw"], [PS(2)])
                    mm(ps[2][:, h * 97:(h + 1) * 97], qk3[0:96, h, :], Sbf3[0:96, h, :], False, True, [rs("qk_b"), sr + "bf"], [PS(2)])
                for h in range(4):
                    mm(ps[3][0:96, h * 97:(h + 1) * 97], ktm_b[s][:, h * 96:(h + 1) * 96], vw3[:, h, :], True, True, [rs("ktm_b"), "vw"], [PS(3)])
                P3 = ps[2][:, 0:388].rearrange("p (h e) -> p h e", h=4)
                tt(den.unsqueeze(2), P3[:, :, 96:97], flo.unsqueeze(2), ALU.max, ["flo"], [PS(2), "den"])
                stt(den.unsqueeze(2), P3[:, :, 96:97], -1.0, den.unsqueeze(2), ALU.mult, ALU.max, ["den"], [PS(2), "den"])
                recip(rc, den, ["den"], ["rc"])
                ho = hout[idx % 2]; hres = "hout%d" % (idx % 2)
                tt(ho.rearrange("p (h e) -> p h e", h=4), P3[:, :, 0:96], rc.unsqueeze(2).to_broadcast([128, 4, 96]), ALU.mult, ["rc"], [PS(2), hres])
                stq(HA_d[dd, tau0:tau0 + 128, :], ho, "bs%d" % (idx % 2), [hres], [("HA", dd, c)])
                U3 = ps[3][0:96, 0:388].rearrange("p (h e) -> p h e", h=4)
                st3 = stmp.rearrange("p (h e) -> p h e", h=4)
                tt(st3[0:96], S323[0:96], U3, ALU.add, [sr + "32"], [PS(3), "stmp"])
                tt(S323[0:96], st3[0:96], dcy[0:96, :].unsqueeze(2).to_broadcast([96, 4, 97]), ALU.mult, ["stmp", "dcy"], [sr + "32"])
                cp(Sbf3[0:96], S323[0:96], [sr + "32"], [sr + "bf"], eng="scalar")
                P.skip = "g" not in b_parts
                cp(vcb, vc_t[s], [rs("vc_t")], ["vcb"], eng="gpsimd")
                gq3 = gq_b[s].rearrange("p (h t) -> p h t", h=4); gk3 = gk_b[s].rearrange("p (h t) -> p h t", h=4)
                for h in range(4):
                    mm(ps[4][:, h * 128:(h + 1) * 128], gk3[0:64, h, :], gq3[0:64, h, :], True, True, [rs("gk_b"), rs("gq_b")], [PS(4)])
                tt(ATg3, ps[4].rearrange("p (h t) -> p h t", h=4), mask.unsqueeze(1).to_broadcast([128, 4, 128]), ALU.mult, [mres], [PS(4), "ATg"])
                for h in range(4):
                    mm(ps[5][:, h * 96:(h + 1) * 96], ATg3[:, h, :], vcb[:, h * 96:(h + 1) * 96], True, False, ["ATg", "vcb"], [PS(5)])
                    mm(ps[5][:, h * 96:(h + 1) * 96], gq3[0:64, h, :], Gbf3[0:64, h, :], False, True, [rs("gq_b"), sr + "gbf"], [PS(5)])
                for h in range(4):
                    g = h // 2; hh = h % 2
                    mm(ps[6][0:64, h * 96:(h + 1) * 96], gkt_b[s][:, g * 128 + hh * 64:g * 128 + hh * 64 + 64], vcb[:, h * 96:(h + 1) * 96], True, True, [rs("gkt_b"), "vcb"], [PS(6)])
                go = gout[idx % 2]; gres = "gout%d" % (idx % 2)
                cp(go, ps[5][:, 0:384], [], [PS(5), gres], eng="scalar")
                stq(HC_d[dd, tau0:tau0 + 128, :], go, "bs%d" % (idx % 2), [gres], [("HC", dd, c)])
                U4 = ps[6][0:64, 0:384].rearrange("p (h e) -> p h e", h=4)
                gt3 = gtmp.rearrange("p (h e) -> p h e", h=4)
                tt(gt3[0:64], G323[0:64], U4, ALU.add, [sr + "g32"], [PS(6), "gtmp"])
                tt(G323[0:64], gt3[0:64], dec_b[s][0:64, :].unsqueeze(2).to_broadcast([64, 4, 96]), ALU.mult, ["gtmp", rs("dec_b")], [sr + "g32"])
                cp(Gbf3[0:64], G323[0:64], [sr + "g32"], [sr + "gbf"], eng="scalar")
                P.skip = False
        ar.release(mB)
        P.barrier()
        if stop_after == "B":
            break

        mC = ar.mark()
        woutb = ar.bf16(8 * 1024); woutb3 = woutb.rearrange("p (k c) -> p k c", k=8)
        stage = [ar.f32(2816), ar.f32(2816)]
        load_cast(lambda kc: wout_d[l, kc * 128:(kc + 1) * 128, :], 1024, woutb3, stage, "stgC", "woutb", "wlC")
        gA = ar.f32(384); gC = ar.f32(96); gS = ar.f32(256)
        ld(gA, mng_d[l:l + 1, :].partition_broadcast(128), "c1", ["gA"])
        ld(gC, gng_d[l:l + 1, :].partition_broadcast(128), "c2", ["gC"])
        ld(gS, sng_d[l:l + 1, :].partition_broadcast(128), "c3", ["gS"])
        swst = ar.f32(512); swb = ar.bf16(512); swb3 = swb.rearrange("p (g t) -> p g t", g=4)
        ld(swst, swT_d[:, l * 512:(l + 1) * 512], "c4", ["swst"])
        cp(swb, swst, ["swst"], ["swb"])
        sbT = ar.f32(4)
        ld(sbT, sbT_d[:, l * 4:(l + 1) * 4], "c5", ["sbT"])
        NS = 2
        ha = [[ar.f32(384) for _ in range(2)] for _ in range(NS)]
        hc = [[ar.f32(384) for _ in range(2)] for _ in range(NS)]
        tmc = [ar.f32(896) for _ in range(NS)]
        gcb = [ar.f32(384) for _ in range(NS)]
        xb = [ar.f32(1024) for _ in range(NS)]
        cat = ar.f32(1024); sq = ar.f32(384); sm = ar.f32(16)
        sgm = ar.f32(384); ug = ar.f32(256); vg = ar.f32(256); vbf = ar.bf16(256)
        catT = ar.bf16(1024); catT3 = catT.rearrange("p (k t) -> p k t", k=8)
        xo = [ar.f32(1024), ar.f32(1024)]
        tiles = list(range(NCH)) if not last else list(range(2, NCH))

        def loadsC(ii):
            c = tiles[ii]; s = ii % NS; tau0 = c * 128
            for dd in range(2):
                ld(ha[s][dd], HA_d[dd, tau0:tau0 + 128, :], "cl%d" % s, ["ha%d_%d" % (s, dd)], R=[("HA", dd, c)])
                ld(hc[s][dd], HC_d[dd, tau0:tau0 + 128, :], "cl%d" % s, ["hc%d_%d" % (s, dd)], R=[("HC", dd, c)])
            ld(tmc[s], TM_d[tau0:tau0 + 128, 384:1280], "cl%d" % s, ["tmc%d" % s], R=[("TM", c)])
            ld(gcb[s], TM_d[tau0:tau0 + 128, 1664:2048], "cl%d" % s, ["gcb%d" % s], R=[("TM", c)])
            src, sres = tile_src(l, c)
            ld(xb[s], src, "cl%d" % s, ["xb%d" % s], R=[sres])

        def headnorm(dst, h0, h1, r0, r1, gbc, gres, gate, gateres):
            d3 = dst.rearrange("p (h e) -> p h e", h=4)
            tt(dst, h0, h1, ALU.add, [r0, r1], ["cat"])
            tt(sq, dst, dst, ALU.mult, ["cat"], ["sq"])
            rsum(sm[:, 0:4], sq.rearrange("p (h e) -> p h e", h=4), ["sq"], ["sm"])
            rstd_from_ss(sm[:, 4:8], sm[:, 0:4], 96, ["sm"], ["sm"])
            tt(d3, d3, sm[:, 4:8].unsqueeze(2).to_broadcast([128, 4, 96]), ALU.mult, ["cat", "sm"], ["cat"])
            tt(d3, d3, gbc, ALU.mult, ["cat", gres], ["cat"])
            tt(dst, dst, gate, ALU.mult, ["cat", gateres], ["cat"], eng="gpsimd")

        if tiles:
            loadsC(0)
        for ii in range(len(tiles)):
            c = tiles[ii]; s = ii % NS; tau0 = c * 128
            v = 1 if c < 2 else 0
            if ii + 1 < len(tiles):
                loadsC(ii + 1)
            tres = "tmc%d" % s
            act(sgm, tmc[s][:, 0:384], AF.Sigmoid, [tres], ["sgm"])
            headnorm(cat[:, 0:384], ha[s][0], ha[s][1], "ha%d_0" % s, "ha%d_1" % s, gA.rearrange("p (h e) -> p h e", h=4), "gA", sgm, "sgm")
            act(sgm, gcb[s], AF.Silu, ["gcb%d" % s], ["sgm"])
            headnorm(cat[:, 640:1024], hc[s][0], hc[s][1], "hc%d_0" % s, "hc%d_1" % s, gC.unsqueeze(1).to_broadcast([128, 4, 96]), "gC", sgm, "sgm")
            act(ug, tmc[s][:, 384:640], AF.Gelu_apprx_tanh, [tres], ["ug"])
            act(vg, tmc[s][:, 640:896], AF.Gelu_apprx_tanh, [tres], ["vg"])
            act(sq[:, 0:256], vg, AF.Square, ["vg"], ["sq", "sm"], accum=sm[:, 8:9])
            rstd_from_ss(sm[:, 9:10], sm[:, 8:9], 256, ["sm"], ["sm"])
            ts(vg, vg, sm[:, 9:10], None, ALU.mult, None, ["vg", "sm"], ["vg"])
            tt(vbf, vg, gS, ALU.mult, ["vg", "gS"], ["vbf"])
            for g in range(4):
                mm(ps[2][:, g * 64:(g + 1) * 64], swb3[:, g, :], vbf[:, g * 64:(g + 1) * 64], True, True, ["swb", "vbf"], [PS(2)])
            c3 = cat[:, 384:640].rearrange("p (g e) -> p g e", g=4)
            tt(c3, ps[2][:, 0:256].rearrange("p (g e) -> p g e", g=4), sbT.unsqueeze(2).to_broadcast([128, 4, 64]), ALU.add, ["sbT"], [PS(2), "cat"])
            tt(cat[:, 384:640], cat[:, 384:640], ug, ALU.mult, ["cat", "ug"], ["cat"])
            if CAT_d is not None:
                stq(CAT_d[tau0:tau0 + 128, :], cat, "dbgcat", ["cat"], [("CAT", c)])
            for half in range(2):
                for j in range(4):
                    kc = half * 4 + j
                    tr(ps[half][:, j * 128:(j + 1) * 128], cat[:, kc * 128:(kc + 1) * 128], ident, ["cat", "ident"], [PS(half)])
                cp(catT[:, half * 512:(half + 1) * 512], ps[half][:, :], [], [PS(half), "catT"], eng=("vector" if half == 0 else "scalar"))
            xot = xo[ii % 2]; xres = "xo%d" % (ii % 2)
            for half in range(2):
                b = 3 + half
                for kc in range(8):
                    mm(ps[b][:, :], catT3[:, kc, :], woutb3[:, kc, half * 512:(half + 1) * 512], kc == 0, kc == 7, ["catT", "woutb"], [PS(b)])
                gsl = Gbc[:, (0 * 2 + v) * 1024 + half * 512:(0 * 2 + v) * 1024 + (half + 1) * 512]
                tt(xot[:, half * 512:(half + 1) * 512], ps[b][:, :], gsl, ALU.mult, ["Gbc"], [PS(b), xres])
                tt(xot[:, half * 512:(half + 1) * 512], xot[:, half * 512:(half + 1) * 512], xb[s][:, half * 512:(half + 1) * 512], ALU.add,
                   [xres, "xb%d" % s], [xres], eng="gpsimd")
            dst, dres = tile_dst(l, c)
            stq(dst, xot, "cs%d" % (ii % 2), [xres], [dres])
        ar.release(mC)
        P.barrier()
        if stop_after == "C":
            break

        mD = ar.mark()
        wfib = ar.bf16(8 * 2 * DFF); wfib3 = wfib.rearrange("p (k c) -> p k c", k=8)
        wfob = ar.bf16(22 * 1024); wfob3 = wfob.rearrange("p (k c) -> p k c", k=22)
        actT = ar.bf16(22 * 512); actT3 = actT.rearrange("p (j t) -> p j t", j=22)
        stage = [actT.bitcast(F32)[:, 0:2816], actT.bitcast(F32)[:, 2816:5632]]
        load_cast(lambda kc: wfi_d[l, kc * 128:(kc + 1) * 128, :], 2 * DFF, wfib3, stage, "actT", "wfib", "wlD")
        load_cast(lambda kc: wfo_d[l, kc * 128:(kc + 1) * 128, :], 1024, wfob3, stage, "actT", "wfob", "wlD", kchunks=22)
        P.barrier()
        xbuf = [ar.f32(1024), ar.f32(1024)]
        xn = ar.f32(1024); small = ar.f32(16)
        hTb = ar.bf16(8 * 512); hT3 = hTb.rearrange("p (k c) -> p k c", k=8)
        sgt = [ar.f32(512)] * 2
        _xo = ar.f32(1024)
        xo = [_xo, _xo]
        fgb = ar.f32(1024)
        if last:
            ld(fgb, fg_d.partition_broadcast(128), "c1", ["fgb"])
        segsD = [sg for sg in segs if not (last and sg[0] < 2)]
        gi_ = 0
        for (c0, nt) in segsD:
            N = nt * 128
            v = 1 if c0 < 2 else 0
            for i in range(nt):
                src, sres = tile_dst(l, c0 + i)
                xt = xbuf[(c0 + i) % 2]; xres = "xbuf%d" % ((c0 + i) % 2)
                ld(xt, src, "xl" + str((c0 + i) % 2), [xres], R=[sres])
                norm_to_hT(xt, xres, hT3, i * 128, 2, 3, v, "")
            for j in range(22):
                bg = 2 + gi_ % 2; bu = 4 + gi_ % 2
                for kc in range(8):
                    mm(ps[bg][:, 0:N], wfib3[:, kc, j * 128:(j + 1) * 128], hT3[:, kc, 0:N], kc == 0, kc == 7, ["wfib", "hT"], [PS(bg)])
                for kc in range(8):
                    mm(ps[bu][:, 0:N], wfib3[:, kc, DFF + j * 128:DFF + (j + 1) * 128], hT3[:, kc, 0:N], kc == 0, kc == 7, ["wfib", "hT"], [PS(bu)])
                sg_t = sgt[0]; sgres = "sgt0"
                act(sg_t[:, 0:N], ps[bg][:, 0:N], AF.Silu, [], [PS(bg), sgres])
                tt(actT3[:, j, 0:N], ps[bu][:, 0:N], sg_t[:, 0:N], ALU.mult, [sgres], [PS(bu), "actT"])
                gi_ += 1
            for i in range(nt):
                c = c0 + i
                src, sres = tile_dst(l, c)
                xt = xbuf[c % 2]; xres = "xbuf%d" % (c % 2)
                ld(xt, src, "xl" + str(c % 2), [xres], R=[sres])
                xot = xo[0]; xores = "xoD0"
                for half in range(2):
                    b = 6 + half
                    for j in range(22):
                        mm(ps[b][:, :], actT3[:, j, i * 128:(i + 1) * 128], wfob3[:, j, half * 512:(half + 1) * 512], j == 0, j == 21, ["actT", "wfob"], [PS(b)])
                    gsl = Gbc[:, (1 * 2 + v) * 1024 + half * 512:(1 * 2 + v) * 1024 + (half + 1) * 512]
                    tt(xot[:, half * 512:(half + 1) * 512], ps[b][:, :], gsl, ALU.mult, ["Gbc"], [PS(b), xores])
                    tt(xot[:, half * 512:(half + 1) * 512], xot[:, half * 512:(half + 1) * 512], xt[:, half * 512:(half + 1) * 512], ALU.add,
                       [xores, xres], [xores], eng="gpsimd")
                if last:
                    act(xn, xot, AF.Square, [xores], ["xn", "small"], accum=small[:, 2:3])
                    rstd_from_ss(small[:, 3:4], small[:, 2:3], D, ["small"], ["small"])
                    ts(xot, xot, small[:, 3:4], None, ALU.mult, None, [xores, "small"], [xores])
                    tt(xot, xot, fgb, ALU.mult, [xores, "fgb"], [xores])
                dst, dres = tile_dst(l, c, final=last)
                stq(dst, xot, "ds%d" % (c % 2), [xores], [dres])
        ar.release(mD)

    P.barrier()
    P.final_wait("gpsimd", [("y", j) for j in range(64)])
    P.emit()
    return nc, P


def host_layout(inputs, b, depth=DEPTH):
    L = depth
    f = lambda a: np.ascontiguousarray(np.asarray(a, dtype=np.float32))
    m = {}
    m["x"] = f(inputs["x"][b]); m["ctx"] = f(inputs["ctx"][b])
    s_in = np.zeros((128, 16), np.float32)
    s_in[:, 0::2] = np.asarray(inputs["c"][b]).reshape(8, 128).T
    s_in[:, 1::2] = np.asarray(inputs["c_ctx"]).reshape(8, 128).T
    m["s_in"] = s_in
    m["w_ada"] = f(inputs["w_ada"][:L])
    m["b_ada"] = f(inputs["b_ada"][:L])
    m["b_ada_fm"] = f(np.asarray(inputs["b_ada"][:L]).reshape(L, 48, 128).transpose(2, 0, 1).reshape(128, L * 48))
    m["norm1_fm"] = f(np.asarray(inputs["norm1_g"][:L]).reshape(L, 8, 128).transpose(2, 0, 1).reshape(128, L * 8))
    m["norm2_fm"] = f(np.asarray(inputs["norm2_g"][:L]).reshape(L, 8, 128).transpose(2, 0, 1).reshape(128, L * 8))
    m["w_in"] = f(inputs["w_in"][:L]); m["w_out"] = f(inputs["w_out"][:L])
    m["w_ffn_in"] = f(inputs["w_ffn_in"][:L]); m["w_ffn_out"] = f(inputs["w_ffn_out"][:L])
    cv = np.asarray(inputs["mlstm_conv"][:L])
    m["conv_fm"] = f(cv.reshape(L, 3, 8, 96).transpose(3, 0, 2, 1).reshape(96, L * 24))
    m["gate_b"] = f(np.asarray(inputs["mlstm_gate_b"][:L]).reshape(L, 16))
    m["mlstm_norm_g"] = f(inputs["mlstm_norm_g"][:L]); m["gla_norm_g"] = f(inputs["gla_norm_g"][:L])
    m["sgu_norm_g"] = f(inputs["sgu_norm_g"][:L])
    wg = np.asarray(inputs["gla_w_gk2"][:L])
    wp = np.zeros((16, L, 2, 2, 2, 64), np.float32)
    wp[..., :48] = wg.reshape(L, 2, 16, 2, 2, 48).transpose(2, 0, 1, 3, 4, 5)
    m["wgk2_pad"] = f(wp.reshape(16, L * 4 * 128))
    bg = np.asarray(inputs["gla_b_gk"][:L])
    bp = np.zeros((2, 64, L, 2, 2), np.float32)
    bp[:, :48] = bg.reshape(L, 2, 2, 2, 48).transpose(3, 4, 0, 1, 2)
    m["bgk_pad"] = f(bp.reshape(128, L * 4))
    sw = np.asarray(inputs["sgu_w"][:L])
    m["sgu_wT"] = f(sw.transpose(3, 0, 1, 2).reshape(128, L * 4 * 128))
    m["sgu_bT"] = f(np.asarray(inputs["sgu_b"][:L]).transpose(2, 0, 1).reshape(128, L * 4))
    m["final_g"] = f(np.asarray(inputs["final_g"]).reshape(1, D))
    m["ident"] = np.eye(128, dtype=np.float32)
    m["maskF"] = np.triu(np.ones((128, 128), np.float32))
    m["maskB"] = np.tril(np.ones((128, 128), np.float32))
    r = np.ones((128, 512), np.float32); r[:, 0::128] = 0.0
    m["resetmask"] = r
    return m


_CACHE = {}


def kernel(**inputs):
    if "nc" not in _CACHE:
        _CACHE["nc"] = build_program(DEPTH)[0]
    nc = _CACHE["nc"]
    in_maps = [host_layout(inputs, b) for b in range(8)]
    res = run_bass_kernel_spmd(nc, in_maps, core_ids=list(range(8)))
    return np.stack([np.asarray(r["y"], dtype=np.float32) for r in res.results], axis=0)
```
